# Optimizing a Trainium2 kernel written in Bass

```python
import jax, jax.numpy as jnp
from jax import lax
import numpy as np

D_MODEL = 2048
BATCH = 4
SEQ = 2048
DEPTH = 4

GRID_W = 64
CTX_LEN = 256
N_MIXERS = 3
NORM_EPS = 1e-6
RET_HEADS = 8
RET_DK = D_MODEL // RET_HEADS
RET_DV = 2 * RET_DK
RET_QK = RET_HEADS * RET_DK
RET_V = RET_HEADS * RET_DV
RET_CHUNK = 128
ROPE_BASE = 10000.0
GM_WIDTH = 2 * D_MODEL
GM_GROUPS = 8
GM_CHUNK = 128
CV_WIDTH = 2 * D_MODEL
CONV_K = 3

kernel_name = "hybrid_retention_gmlp_shortconv_dit"


def _rmsnorm(x, g):
    xf = x.astype(jnp.float32)
    y = xf * lax.rsqrt(jnp.mean(xf * xf, axis=-1, keepdims=True) + NORM_EPS)
    return y.astype(x.dtype) * g


def _layernorm(x):
    xf = x.astype(jnp.float32)
    mu = jnp.mean(xf, axis=-1, keepdims=True)
    var = jnp.mean(jnp.square(xf - mu), axis=-1, keepdims=True)
    return ((xf - mu) * lax.rsqrt(var + NORM_EPS)).astype(x.dtype)


def _split_heads(a, n_heads):
    b, t, _ = a.shape
    return a.reshape(b, t, n_heads, -1).transpose(0, 2, 1, 3)


def _merge_heads(a):
    b, h, t, d = a.shape
    return a.transpose(0, 2, 1, 3).reshape(b, t, h * d)


def _rope1d(x, pos):
    half = x.shape[-1] // 2
    freqs = ROPE_BASE ** (-jnp.arange(half, dtype=jnp.float32) / half)
    ang = pos.astype(jnp.float32)[:, None] * freqs[None, :]
    cos = jnp.cos(ang).astype(x.dtype)
    sin = jnp.sin(ang).astype(x.dtype)
    x1, x2 = x[..., :half], x[..., half:]
    return jnp.concatenate([x1 * cos - x2 * sin, x1 * sin + x2 * cos], axis=-1)


def _axial_rope(x, rows, cols):
    half = x.shape[-1] // 2
    return jnp.concatenate([_rope1d(x[..., :half], rows), _rope1d(x[..., half:], cols)], axis=-1)


def _maybe_flip(a, flip):
    return jnp.flip(a, axis=2) if flip else a


def _ret_chunk_scan(q, k, v, log_g, s0):
    b, h, t, _ = q.shape
    n = t // RET_CHUNK
    lg = log_g.astype(jnp.float32)
    idx = jnp.arange(RET_CHUNK, dtype=jnp.float32)
    diff = idx[:, None] - idx[None, :]
    intra = jnp.where(diff >= 0, jnp.exp(lg[:, None, None] * jnp.maximum(diff, 0.0)), 0.0).astype(q.dtype)
    q_dec = jnp.exp(lg[:, None] * (idx + 1.0)).astype(q.dtype)
    k_dec = jnp.exp(lg[:, None] * (RET_CHUNK - 1.0 - idx)).astype(q.dtype)
    chunk_dec = jnp.exp(lg * RET_CHUNK).astype(q.dtype)

    def split(a):
        return jnp.moveaxis(a.reshape(b, h, n, RET_CHUNK, a.shape[-1]), 2, 0)

    def step(s, inp):
        qc, kc, vc = inp
        scores = jnp.einsum('bhid,bhjd->bhij', qc, kc) * intra[None]
        o = (jnp.einsum('bhij,bhje->bhie', scores, vc)
             + jnp.einsum('bhid,bhde->bhie', qc, s) * q_dec[None, :, :, None])
        s = (s * chunk_dec[None, :, None, None]
             + jnp.einsum('bhjd,bhje->bhde', kc * k_dec[None, :, :, None], vc))
        return s, o

    s_fin, o = lax.scan(step, s0, (split(q), split(k), split(v)))
    o = jnp.moveaxis(o, 0, 2).reshape(b, h, t, v.shape[-1])
    return o, s_fin


def _ret_final_state(k, v, log_g):
    t = k.shape[2]
    w = jnp.exp(log_g.astype(jnp.float32)[:, None]
                * (t - 1.0 - jnp.arange(t, dtype=jnp.float32))[None, :]).astype(k.dtype)
    return jnp.einsum('bhtd,bhte->bhde', k * w[None, :, :, None], v)


def _retention_mixer(h_ctx, h_lat, rows, cols, w_in, w_out, decay_logit, need_ctx):
    scale = RET_DK ** -0.5
    q_l, k_l, v_l, z_l = jnp.split(h_lat @ w_in, [RET_QK, 2 * RET_QK, 2 * RET_QK + RET_V], axis=-1)
    q_l = _axial_rope(_split_heads(q_l, RET_HEADS), rows, cols)
    k_l = _axial_rope(_split_heads(k_l, RET_HEADS) * scale, rows, cols)
    v_l = _split_heads(v_l, RET_HEADS)
    if need_ctx:
        q_c, k_c, v_c, z_c = jnp.split(h_ctx @ w_in, [RET_QK, 2 * RET_QK, 2 * RET_QK + RET_V], axis=-1)
        q_c = _split_heads(q_c, RET_HEADS)
    else:
        k_c, v_c = jnp.split(h_ctx @ w_in[:, RET_QK:2 * RET_QK + RET_V], [RET_QK], axis=-1)
    k_c = _split_heads(k_c, RET_HEADS) * scale
    v_c = _split_heads(v_c, RET_HEADS)

    log_g = jax.nn.log_sigmoid(decay_logit.astype(jnp.float32))
    o_lat_dirs = []
    o_ctx_dirs = []
    for d in range(2):
        flip = d == 1
        if need_ctx:
            s0 = jnp.zeros((k_c.shape[0], RET_HEADS, RET_DK, RET_DV), k_c.dtype)
            oc, s_ctx = _ret_chunk_scan(_maybe_flip(q_c, flip), _maybe_flip(k_c, flip),
                                        _maybe_flip(v_c, flip), log_g[d], s0)
            o_ctx_dirs.append(_maybe_flip(oc, flip))
        else:
            s_ctx = _ret_final_state(_maybe_flip(k_c, flip), _maybe_flip(v_c, flip), log_g[d])
        ol, _ = _ret_chunk_scan(_maybe_flip(q_l, flip), _maybe_flip(k_l, flip),
                                _maybe_flip(v_l, flip), log_g[d], s_ctx)
        o_lat_dirs.append(_maybe_flip(ol, flip))

    def finish(o, z):
        o = _merge_heads(_layernorm(o))
        return (jax.nn.silu(z) * o) @ w_out

    y_lat = finish(o_lat_dirs[0] + o_lat_dirs[1], z_l)
    y_ctx = finish(o_ctx_dirs[0] + o_ctx_dirs[1], z_c) if need_ctx else None
    return y_ctx, y_lat


def _spatial_gate(u, v, v_g, w_s, b_s):
    b, t, e = v.shape
    v = _layernorm(v) * v_g
    vb = v.reshape(b, t // GM_CHUNK, GM_CHUNK, GM_GROUPS, e // GM_GROUPS)
    s = jnp.einsum('gij,bnjgc->bnigc', w_s, vb) + b_s.T[None, None, :, :, None]
    return u * s.reshape(b, t, e)


def _gmlp_mixer(h_ctx, h_lat, w_in, v_g, w_s, b_s, w_out, need_ctx):
    def branch(h):
        u, v, z = jnp.split(h @ w_in, 3, axis=-1)
        y = _spatial_gate(jax.nn.gelu(u), jax.nn.gelu(v), v_g, w_s, b_s)
        return (jax.nn.silu(z) * y) @ w_out
    return (branch(h_ctx) if need_ctx else None), branch(h_lat)


def _dwconv3(a, w, bias):
    y = lax.conv_general_dilated(a, w[:, None, :], window_strides=(1,), padding=[(1, 1)],
                                 dimension_numbers=('NWC', 'WIO', 'NWC'),
                                 feature_group_count=a.shape[-1])
    return y + bias


def _shortconv_mixer(h_ctx, h_lat, w_in, conv_w, conv_b, w_out, need_ctx):
    def branch(h):
        bg, cg, hh, z = jnp.split(h @ w_in, 4, axis=-1)
        y = bg * _dwconv3(cg * hh, conv_w, conv_b)
        return (jax.nn.silu(z) * y) @ w_out
    return (branch(h_ctx) if need_ctx else None), branch(h_lat)


def setup_inputs(seed: int = 0) -> dict:
    key = jax.random.key(seed)
    ks = jax.random.split(key, 20)
    f32 = jnp.float32
    n_a = len(range(0, DEPTH, N_MIXERS))
    n_b = len(range(1, DEPTH, N_MIXERS))
    n_c = len(range(2, DEPTH, N_MIXERS))

    def nrm(k, shape, s):
        return jax.random.normal(k, shape, f32) * s

    gamma0 = 1.0 - 2.0 ** (-5.0 - jnp.arange(RET_HEADS, dtype=f32))
    return {
        "x": nrm(ks[0], (BATCH, SEQ, D_MODEL), 1.0),
        "c": nrm(ks[1], (BATCH, D_MODEL), 1.0),
        "ctx": nrm(ks[2], (BATCH, CTX_LEN, D_MODEL), 1.0),
        "c_ctx": nrm(ks[3], (D_MODEL,), 1.0),
        "norm_g": 1.0 + nrm(ks[4], (DEPTH, D_MODEL), 0.02),
        "ada_w": nrm(ks[5], (DEPTH, D_MODEL, 3 * D_MODEL), 0.5 * D_MODEL ** -0.5),
        "ada_b": nrm(ks[6], (DEPTH, 3 * D_MODEL), 0.02),
        "final_g": 1.0 + nrm(ks[7], (D_MODEL,), 0.02),
        "ret_w_in": nrm(ks[8], (n_a, D_MODEL, 2 * RET_QK + 2 * RET_V), D_MODEL ** -0.5),
        "ret_w_out": nrm(ks[9], (n_a, RET_V, D_MODEL), RET_V ** -0.5),
        "ret_decay": jnp.log(gamma0 / (1.0 - gamma0)) + nrm(ks[10], (n_a, 2, RET_HEADS), 0.1),
        "gm_w_in": nrm(ks[11], (n_b, D_MODEL, 3 * GM_WIDTH), D_MODEL ** -0.5),
        "gm_v_g": 1.0 + nrm(ks[12], (n_b, GM_WIDTH), 0.02),
        "gm_w_s": nrm(ks[13], (n_b, GM_GROUPS, GM_CHUNK, GM_CHUNK), GM_CHUNK ** -0.5),
        "gm_b_s": 1.0 + nrm(ks[14], (n_b, GM_GROUPS, GM_CHUNK), 0.02),
        "gm_w_out": nrm(ks[15], (n_b, GM_WIDTH, D_MODEL), GM_WIDTH ** -0.5),
        "cv_w_in": nrm(ks[16], (n_c, D_MODEL, 4 * CV_WIDTH), D_MODEL ** -0.5),
        "cv_conv_w": nrm(ks[17], (n_c, CONV_K, CV_WIDTH), CONV_K ** -0.5),
        "cv_conv_b": nrm(ks[18], (n_c, CV_WIDTH), 0.01),
        "cv_w_out": nrm(ks[19], (n_c, CV_WIDTH, D_MODEL), CV_WIDTH ** -0.5),
    }


def reference(x, c, ctx, c_ctx, norm_g, ada_w, ada_b, final_g,
              ret_w_in, ret_w_out, ret_decay,
              gm_w_in, gm_v_g, gm_w_s, gm_b_s, gm_w_out,
              cv_w_in, cv_conv_w, cv_conv_b, cv_w_out):
    n_lat = x.shape[1]
    ROWS = n_lat // GRID_W
    rows = jnp.repeat(jnp.arange(ROWS), GRID_W)
    cols = jnp.tile(jnp.arange(GRID_W), ROWS)
    sc = jax.nn.silu(c)
    sc_ctx = jax.nn.silu(c_ctx)
    ia = ib = ic = 0
    for i in range(DEPTH):
        need_ctx = i < DEPTH - 1
        shift, scale, gate = jnp.split(sc @ ada_w[i] + ada_b[i], 3, axis=-1)
        shift_c, scale_c, gate_c = jnp.split(sc_ctx @ ada_w[i] + ada_b[i], 3, axis=-1)
        h_lat = _rmsnorm(x, norm_g[i]) * (1.0 + scale[:, None]) + shift[:, None]
        h_ctx = _rmsnorm(ctx, norm_g[i]) * (1.0 + scale_c) + shift_c
        kind = i % N_MIXERS
        if kind == 0:
            y_ctx, y_lat = _retention_mixer(h_ctx, h_lat, rows, cols, ret_w_in[ia], ret_w_out[ia],
                                            ret_decay[ia], need_ctx)
            ia += 1
        elif kind == 1:
            y_ctx, y_lat = _gmlp_mixer(h_ctx, h_lat, gm_w_in[ib], gm_v_g[ib], gm_w_s[ib], gm_b_s[ib],
                                       gm_w_out[ib], need_ctx)
            ib += 1
        else:
            y_ctx, y_lat = _shortconv_mixer(h_ctx, h_lat, cv_w_in[ic], cv_conv_w[ic], cv_conv_b[ic],
                                            cv_w_out[ic], need_ctx)
            ic += 1
        x = x + gate[:, None] * y_lat
        if need_ctx:
            ctx = ctx + gate_c * y_ctx
    return _rmsnorm(x, final_g)
```

```python
import numpy as np
from contextlib import ExitStack
import concourse.bass as bass
import concourse.mybir as mybir
from concourse.bass_utils import run_bass_kernel_spmd

F32 = mybir.dt.float32
BF16 = mybir.dt.bfloat16
I32 = mybir.dt.int32
AF = mybir.ActivationFunctionType
ALU = mybir.AluOpType

D = 2048
KT = 16
NCC = 2
EPS = 1e-6
ENGS = ("pe", "act", "dve", "pool", "sp")


class Res:
    __slots__ = ("name", "last_w", "readers")

    def __init__(self, name):
        self.name = name
        self.last_w = None
        self.readers = []


class Op:
    __slots__ = ("eng", "fn", "deps", "needed", "sig", "is_dma", "slot", "val", "nd")

    def __init__(self, eng, fn):
        self.eng = eng
        self.fn = fn
        self.deps = []
        self.needed = False
        self.sig = 0
        self.is_dma = False
        self.slot = None
        self.val = 0
        self.nd = 1


class Prog:
    def __init__(self, nc, ndma_slots=None):
        self.nc = nc
        self.ops = {e: [] for e in ENGS}
        self.all_res = []
        self.ndma = ndma_slots or {"sp": 4, "pool": 2, "act": 2}
        self.dma_cnt = {"sp": 0, "pool": 0, "act": 0}
        self.dma_slot_last = {}
        self.dma_slot_val = {}
        self.out_dmas = []
        self.pending_dma = []

    def res(self, name):
        r = Res(name)
        self.all_res.append(r)
        return r

    def _track(self, op, reads, writes):
        deps = []
        for r in reads:
            if r.last_w is not None:
                deps.append(r.last_w)
        for w in writes:
            if w.last_w is not None:
                deps.append(w.last_w)
            deps.extend(w.readers)
        seen = set()
        for d in deps:
            if d is op or id(d) in seen:
                continue
            seen.add(id(d))
            if d.eng == "pe" and op.eng == "pe" and not d.is_dma:
                continue
            op.deps.append(d)
            d.needed = True
        for r in reads:
            r.readers.append(op)
        for w in writes:
            w.last_w = op
            w.readers = []

    def op(self, eng, fn, reads=(), writes=()):
        o = Op(eng, fn)
        self._track(o, reads, writes)
        self.ops[eng].append(o)
        return o

    def seq(self, eng, fns, reads=(), writes=()):
        tmp = self.res("seq")
        o = None
        for i, fn in enumerate(fns):
            last = (i == len(fns) - 1)
            o = self.op(eng, fn, reads=list(reads) if i == 0 else [tmp],
                        writes=([tmp] + (list(writes) if last else [])))
        return o

    def dma(self, q, pairs, reads=(), writes=()):
        MAXSUB = 4
        if len(pairs) > MAXSUB:
            o = None
            for i in range(0, len(pairs), MAXSUB):
                o = self._dma1(q, pairs[i:i + MAXSUB], reads, writes, first=(i == 0))
            return o
        return self._dma1(q, pairs, reads, writes, first=True)

    def _dma1(self, q, pairs, reads, writes, first):
        def fn(e, pairs=pairs):
            return [e.dma_start(out=o_, in_=i_) for (o_, i_) in pairs]
        o = Op(q, fn)
        o.is_dma = True
        o.nd = len(pairs)
        k = self.dma_cnt[q]
        self.dma_cnt[q] += 1
        nsl = self.ndma[q]
        slot = (q, k % nsl)
        o.slot = slot
        prev = self.dma_slot_last.get(slot)
        self.dma_slot_val[slot] = self.dma_slot_val.get(slot, 0) + 16 * o.nd
        o.val = self.dma_slot_val[slot]
        self._track(o, reads, writes)
        if prev is not None and prev not in o.deps:
            o.deps.append(prev)
        self.dma_slot_last[slot] = o
        self.ops[q].append(o)
        self.pending_dma.append(o)
        return o

    def barrier(self):
        lasts = []
        for e in ENGS:
            for o in reversed(self.ops[e]):
                if not o.is_dma:
                    lasts.append(o)
                    break
        dmas = list(self.pending_dma)
        self.pending_dma = []
        for e in ENGS:
            o = Op(e, lambda eng: eng.nop())
            for d in lasts:
                if d.eng != e:
                    o.deps.append(d)
                    d.needed = True
            o.deps.extend(dmas)
            self.ops[e].append(o)
        for r in self.all_res:
            r.last_w = None
            r.readers = []

    def emit(self, stack):
        nc = self.nc
        sem_e = {e: stack.enter_context(nc.semaphore("s_" + e)) for e in ENGS}
        sem_d = {}
        for q in ("sp", "pool", "act"):
            if self.dma_cnt[q] == 0:
                continue
            for i in range(min(self.ndma[q], self.dma_cnt[q])):
                sem_d[(q, i)] = stack.enter_context(nc.semaphore("d_%s%d" % (q, i)))
        for e in ENGS:
            c = 0
            for o in self.ops[e]:
                if (not o.is_dma) and o.needed:
                    c += 1
                    o.sig = c
        block = stack.enter_context(nc.Block())

        def run(e, eng):
            waited = {}
            for o in self.ops[e]:
                need = {}
                for d in o.deps:
                    if d.is_dma:
                        key, v = ("d", d.slot), d.val
                    else:
                        key, v = ("e", d.eng), d.sig
                    if v > need.get(key, 0):
                        need[key] = v
                for key, v in need.items():
                    if waited.get(key, 0) >= v:
                        continue
                    waited[key] = v
                    sem = sem_d[key[1]] if key[0] == "d" else sem_e[key[1]]
                    eng.wait_ge(sem, v)
                ins = o.fn(eng)
                if o.is_dma:
                    for i_ in ins:
                        i_.then_inc(sem_d[o.slot], 16)
                elif o.needed:
                    if isinstance(ins, (list, tuple)):
                        ins = ins[-1]
                    ins.then_inc(sem_e[e], 1)
            if e in ("sp", "pool", "act"):
                for (q, i), s in sem_d.items():
                    if q == e:
                        eng.wait_ge(s, self.dma_slot_val[(q, i)])

        @block.tensor
        def _(eng):
            run("pe", eng)

        @block.scalar
        def _(eng):
            run("act", eng)

        @block.vector
        def _(eng):
            run("dve", eng)

        @block.gpsimd
        def _(eng):
            run("pool", eng)

        @block.sync
        def _(eng):
            run("sp", eng)


def build(NCL, layers, debug_x=False, final_norm=True):
    NCH = NCC + NCL
    NT = NCH * 128
    NLAT = NCL * 128
    nc = bass.Bass("TRN2", target_bir_lowering=False)
    P = Prog(nc)
    L = len(layers)
    n_ret = sum(1 for k, _, _ in layers if k == 0)
    n_gm = sum(1 for k, _, _ in layers if k == 1)
    n_cv = sum(1 for k, _, _ in layers if k == 2)

    def din(name, shape, dt=F32):
        return nc.dram_tensor(name, list(shape), dt, kind="ExternalInput").ap()

    def dscr(name, shape, dt=F32):
        return nc.dram_tensor(name, list(shape), dt, kind="Internal").ap()

    xin = din("xin", [NT, D])
    cT = din("cT", [128, 2 * KT])
    norm_g = din("norm_g", [L, D])
    ada_w = din("ada_w", [L, D, 3 * D])
    ada_b = din("ada_b", [L, 3 * D])
    final_g = din("final_g", [1, D])
    ropeT = din("ropeT", [128, max(NCL, 1), 2, 2, 64])
    if n_ret:
        ret_w_in = din("ret_w_in", [n_ret, D, 12288])
        ret_w_out = din("ret_w_out", [n_ret, 4096, D])
        ret_decay = din("ret_decay", [n_ret, 16])
    if n_gm:
        gm_w_in = din("gm_w_in", [n_gm, D, 12288])
        gm_w_out = din("gm_w_out", [n_gm, 4096, D])
        gm_v_g = din("gm_v_g", [n_gm, 4096])
        gm_wsT = din("gm_wsT", [n_gm, 128, 8, 128])
        gm_b_s = din("gm_b_s", [n_gm, 8 * 128])
    if n_cv:
        cv_w_in = din("cv_w_in", [n_cv, D, 16384])
        cv_w_out = din("cv_w_out", [n_cv, 4096, D])
        cv_cw = din("cv_cw", [n_cv, 128, 32, 4])
    out = nc.dram_tensor("out", [NLAT, D], F32, kind="ExternalOutput").ap()
    if debug_x:
        xdbg = nc.dram_tensor("xdbg", [NT, D], F32, kind="ExternalOutput").ap()

    X = dscr("X", [NT, D])
    HT = dscr("HT", [NCH, 128, KT, 128], BF16)
    GT = dscr("GT", [NCH, 128, 32, 128], BF16)
    MODS = dscr("MODS", [2, 3 * D])
    OFS = dscr("OFS", [NCH, 128, 512])
    VG = dscr("VG", [NCH, 128, 4096], BF16)
    ST = dscr("ST", [NCH, 128, 32, 128], BF16)
    rX = [P.res("X%d" % c) for c in range(NCH)]
    rHT = [P.res("HT%d" % c) for c in range(NCH)]
    rGT = [P.res("GT%d" % c) for c in range(NCH)]
    rMODS = P.res("MODS")
    rOFS = [P.res("OFS%d" % c) for c in range(NCH)]
    rVG = [P.res("VG%d" % c) for c in range(NCH)]
    rST = [P.res("ST%d" % c) for c in range(NCH)]

    top = ExitStack()

    uid = [0]

    def sb(stack, name, shape, dt):
        uid[0] += 1
        t = stack.enter_context(nc.sbuf_tensor("%s_%d" % (name, uid[0]), list(shape), dt))
        return t

    ident = sb(top, "ident", [128, 128], BF16)
    r_ident = P.res("ident")
    WB = [sb(top, "wb%d" % i, [128, KT, 512], BF16) for i in range(3)]
    rWB = [P.res("wb%d" % i) for i in range(3)]
    screp = sb(top, "screp", [128, 2, KT, 128], BF16)
    r_screp = P.res("screp")
    PS = [top.enter_context(nc.psum_tensor("ps%d" % i, [128, 512], F32)) for i in range(7)]
    rPS = [P.res("ps%d" % i) for i in range(7)]
    PTb = top.enter_context(nc.psum_tensor("ptb", [128, 1024], BF16))
    rPT = [P.res("ptb0"), P.res("ptb1")]
    ps_rr = [0]

    def next_ps():
        i = ps_rr[0] % 7
        ps_rr[0] += 1
        return i

    wb_rr = [0]

    def next_wb():
        i = wb_rr[0] % 3
        wb_rr[0] += 1
        return i

    with ExitStack() as st:
        iot = sb(st, "iot", [128, 128], I32)
        iof = sb(st, "iof", [128, 128], F32)
        r_i = P.res("iot")
        P.op("pool", lambda e: e.iota(iot[:], pattern=[[1, 128]], base=0, channel_multiplier=-1),
             writes=[r_i])
        P.op("dve", lambda e: e.tensor_copy(out=iof[:], in_=iot[:]), reads=[r_i], writes=[r_i])
        P.op("dve", lambda e: e.tensor_single_scalar(out=ident[:], in_=iof[:], scalar=0.0, op=ALU.is_equal),
             reads=[r_i], writes=[r_ident])
        ct = sb(st, "ct", [128, 2 * KT], F32)
        sg = sb(st, "sg", [128, 2 * KT], F32)
        sc = sb(st, "sc", [128, 2 * KT], F32)
        r_ct = P.res("ct")
        P.dma("sp", [(ct[:], cT)], writes=[r_ct])
        P.op("act", lambda e: e.activation(out=sg[:], in_=ct[:], func=AF.Sigmoid), reads=[r_ct], writes=[r_ct])
        P.op("dve", lambda e: e.tensor_tensor(out=sc[:], in0=sg[:], in1=ct[:], op=ALU.mult),
             reads=[r_ct], writes=[r_ct])
        for s in range(2):
            P.op("dve", lambda e, s=s: e.tensor_copy(
                out=screp[:, s, :, :],
                in_=sc[:, s * KT:(s + 1) * KT].unsqueeze(2).to_broadcast([128, KT, 128])),
                reads=[r_ct], writes=[r_screp])
        for c in range(NCH):
            P.dma("sp", [(X[c * 128:(c + 1) * 128, :], xin[c * 128:(c + 1) * 128, :])], writes=[rX[c]])
        P.barrier()

    def load_wblock(buf_i, segs):
        pairs = []
        for (c0, ncol, src) in segs:
            srcv = src.rearrange("(kt p) c -> p kt c", p=128)
            step = 4 if ncol >= 256 else 8
            for k0 in range(0, KT, step):
                pairs.append((WB[buf_i][:, k0:k0 + step, c0:c0 + ncol], srcv[:, k0:k0 + step, :]))
        return P.dma("pool", pairs, writes=[rWB[buf_i]])

    def rms_stats(x_t, st_t, mv_t, r_x, r_s):
        def f1(e):
            r = None
            for q in range(4):
                r = e.bn_stats(out=st_t[:, q, :], in_=x_t[:, q * 512:(q + 1) * 512])
            return r
        P.seq("dve", [
            f1,
            lambda e: e.bn_aggr(out=mv_t[:, 0:2], in_=st_t[:].rearrange("p a b -> p (a b)")),
            lambda e: e.scalar_tensor_tensor(out=mv_t[:, 2:3], in0=mv_t[:, 0:1], scalar=mv_t[:, 0:1],
                                             in1=mv_t[:, 1:2], op0=ALU.mult, op1=ALU.add),
        ], reads=[r_x], writes=[r_s])
        P.op("dve", lambda e: e.tensor_scalar(out=mv_t[:, 2:3], in0=mv_t[:, 2:3], scalar1=EPS, scalar2=None,
                                              op0=ALU.add), reads=[r_s], writes=[r_s])
        P.op("act", lambda e: e.activation(out=mv_t[:, 3:4], in_=mv_t[:, 2:3], func=AF.Sqrt), reads=[r_s], writes=[r_s])
        P.op("dve", lambda e: e.reciprocal(out=mv_t[:, 3:4], in_=mv_t[:, 3:4]), reads=[r_s], writes=[r_s])

    def ln_stats(x_t, nq, st_t, mv_t, r_x, r_s):
        def f1(e):
            r = None
            if nq == 1:
                return e.bn_stats(out=st_t[:], in_=x_t[:])
            for q in range(nq):
                r = e.bn_stats(out=st_t[:, q, :], in_=x_t[:, q * 512:(q + 1) * 512])
            return r
        P.seq("dve", [
            f1,
            lambda e: e.bn_aggr(out=mv_t[:, 0:2], in_=(st_t[:] if nq == 1 else st_t[:].rearrange("p a b -> p (a b)"))),
        ], reads=[r_x], writes=[r_s])
        P.op("dve", lambda e: e.tensor_scalar(out=mv_t[:, 3:4], in0=mv_t[:, 1:2], scalar1=EPS, scalar2=None,
                                              op0=ALU.add), reads=[r_s], writes=[r_s])
        P.op("act", lambda e: e.activation(out=mv_t[:, 2:3], in_=mv_t[:, 3:4], func=AF.Sqrt), reads=[r_s], writes=[r_s])
        P.op("dve", lambda e: e.reciprocal(out=mv_t[:, 2:3], in_=mv_t[:, 2:3]), reads=[r_s], writes=[r_s])

    def phase_mod(li):
        with ExitStack() as st:
            adab = sb(st, "adab", [1, 3 * D], F32)
            gb = sb(st, "gb", [1, D], F32)
            mo = sb(st, "mo", [1, 2, 3 * D], F32)
            r_ab = P.res("adab")
            r_mo = P.res("mo")
            P.dma("sp", [(adab[:], ada_b[li:li + 1, :]), (gb[:], norm_g[li:li + 1, :])], writes=[r_ab])
            for n in range(12):
                bi = next_wb()
                load_wblock(bi, [(0, 512, ada_w[li, :, n * 512:(n + 1) * 512])])
                for s in range(2):
                    pi = next_ps()

                    def mm(e, s=s, pi=pi, bi=bi):
                        r = None
                        for kt in range(KT):
                            r = e.matmul(PS[pi][:], lhsT=screp[:, s, kt, :], rhs=WB[bi][:, kt, :],
                                         start=(kt == 0), stop=(kt == KT - 1))
                        return r
                    P.op("pe", mm, reads=[r_screp, rWB[bi]], writes=[rPS[pi]])
                    cs = slice(n * 512, (n + 1) * 512)
                    if 4 <= n < 8:
                        gs = slice((n - 4) * 512, (n - 3) * 512)
                        P.op("dve", lambda e, s=s, pi=pi, cs=cs: e.tensor_tensor(
                            out=mo[:, s, cs], in0=PS[pi][0:1, :], in1=adab[:, cs], op=ALU.add),
                            reads=[rPS[pi], r_ab], writes=[r_mo])
                        P.op("dve", lambda e, s=s, cs=cs, gs=gs: e.scalar_tensor_tensor(
                            out=mo[:, s, cs], in0=mo[:, s, cs], scalar=1.0, in1=gb[:, gs],
                            op0=ALU.add, op1=ALU.mult), reads=[r_mo, r_ab], writes=[r_mo])
                    else:
                        P.op("dve", lambda e, s=s, pi=pi, cs=cs: e.tensor_tensor(
                            out=mo[:, s, cs], in0=PS[pi][0:1, :], in1=adab[:, cs], op=ALU.add),
                            reads=[rPS[pi], r_ab], writes=[r_mo])
            P.dma("sp", [(MODS[0:1, :], mo[:, 0, :]), (MODS[1:2, :], mo[:, 1, :])], reads=[r_mo], writes=[rMODS])
            P.barrier()

    def phase_norm(li, chunks):
        with ExitStack() as st:
            AB = sb(st, "AB", [128, 2, 2, D], F32)
            r_AB = P.res("AB")
            prs = []
            for s in range(2):
                prs.append((AB[:, s, 0, :], MODS[s:s + 1, D:2 * D].partition_broadcast(128)))
                prs.append((AB[:, s, 1, :], MODS[s:s + 1, 0:D].partition_broadcast(128)))
            P.dma("sp", prs, reads=[rMODS], writes=[r_AB])
            xt = [sb(st, "xt%d" % i, [128, D], F32) for i in range(2)]
            tt = [sb(st, "tt%d" % i, [128, D], F32) for i in range(2)]
            hb = [sb(st, "hb%d" % i, [128, D], BF16) for i in range(2)]
            hts = [sb(st, "hts%d" % i, [128, KT, 128], BF16) for i in range(2)]
            stt = [sb(st, "stt%d" % i, [128, 4, 6], F32) for i in range(2)]
            mv = [sb(st, "mv%d" % i, [128, 4], F32) for i in range(2)]
            r_xt = [P.res("xt%d" % i) for i in range(2)]
            r_tt = [P.res("tt%d" % i) for i in range(2)]
            r_hb = [P.res("hb%d" % i) for i in range(2)]
            r_hts = [P.res("hts%d" % i) for i in range(2)]
            r_st = [P.res("stt%d" % i) for i in range(2)]
            for n_, c in enumerate(chunks):
                b = n_ % 2
                s = 1 if c < NCC else 0
                P.dma("sp", [(xt[b][:], X[c * 128:(c + 1) * 128, :])], reads=[rX[c]], writes=[r_xt[b]])

                rms_stats(xt[b], stt[b], mv[b], r_xt[b], r_st[b])
                P.op("dve", lambda e, b=b, s=s: e.scalar_tensor_tensor(
                    out=tt[b][:], in0=xt[b][:], scalar=mv[b][:, 3:4], in1=AB[:, s, 0, :],
                    op0=ALU.mult, op1=ALU.mult), reads=[r_xt[b], r_st[b], r_AB], writes=[r_tt[b]])
                P.op("pool", lambda e, b=b, s=s: e.tensor_tensor(
                    out=hb[b][:], in0=tt[b][:], in1=AB[:, s, 1, :], op=ALU.add),
                    reads=[r_tt[b], r_AB], writes=[r_hb[b]])
                for half in range(2):
                    def tr(e, b=b, half=half):
                        r = None
                        for k in range(8):
                            kt = half * 8 + k
                            r = e.transpose(PTb[:, k * 128:(k + 1) * 128], hb[b][:, kt * 128:(kt + 1) * 128], ident[:])
                        return r
                    P.op("pe", tr, reads=[r_hb[b], r_ident], writes=[rPT[0]])
                    P.op("act", lambda e, b=b, half=half: e.copy(
                        out=hts[b][:, half * 8:(half + 1) * 8, :],
                        in_=PTb[:].rearrange("p (k t) -> p k t", k=8)),
                        reads=[rPT[0]], writes=[r_hts[b]])
                P.dma("sp", [(HT[c], hts[b][:])], reads=[r_hts[b]], writes=[rHT[c]])
            P.barrier()

    def phase_out(w_out, chunks):
        with ExitStack() as st:
            W2 = [sb(st, "w2_%d" % i, [128, 32, 512], BF16) for i in range(2)]
            rW2 = [P.res("w2_%d" % i) for i in range(2)]
            Gt = sb(st, "Gt", [128, 2, D], F32)
            r_G = P.res("Gt")
            P.dma("sp", [(Gt[:, s, :], MODS[s:s + 1, 2 * D:3 * D].partition_broadcast(128)) for s in range(2)],
                  reads=[rMODS], writes=[r_G])
            gts = [sb(st, "gts%d" % i, [128, 32, 128], BF16) for i in range(3)]
            r_gts = [P.res("gts%d" % i) for i in range(3)]
            xo = [sb(st, "xo%d" % i, [128, 512], F32) for i in range(3)]
            yo = [sb(st, "yo%d" % i, [128, 512], F32) for i in range(3)]
            r_xo = [P.res("xo%d" % i) for i in range(3)]
            r_yo = [P.res("yo%d" % i) for i in range(3)]
            w_v = w_out.rearrange("(kt p) c -> p kt c", p=128)
            it = 0
            for n in range(4):
                wi = n % 2
                P.dma("pool", [(W2[wi][:, k0:k0 + 4, :], w_v[:, k0:k0 + 4, n * 512:(n + 1) * 512])
                               for k0 in range(0, 32, 4)], writes=[rW2[wi]])
                for c in chunks:
                    b = it % 3
                    it += 1
                    s = 1 if c < NCC else 0
                    P.dma("sp", [(gts[b][:], GT[c])], reads=[rGT[c]], writes=[r_gts[b]])
                    P.dma("sp", [(xo[b][:], X[c * 128:(c + 1) * 128, n * 512:(n + 1) * 512])],
                          reads=[rX[c]], writes=[r_xo[b]])
                    pi = next_ps()

                    def mm(e, b=b, pi=pi, wi=wi):
                        r = None
                        for kt in range(32):
                            r = e.matmul(PS[pi][:], lhsT=gts[b][:, kt, :], rhs=W2[wi][:, kt, :],
                                         start=(kt == 0), stop=(kt == 31))
                        return r
                    P.op("pe", mm, reads=[r_gts[b], rW2[wi]], writes=[rPS[pi]])
                    P.op("dve", lambda e, b=b, pi=pi, s=s, n=n: e.tensor_tensor(
                        out=yo[b][:], in0=PS[pi][:], in1=Gt[:, s, n * 512:(n + 1) * 512], op=ALU.mult),
                        reads=[rPS[pi], r_G], writes=[r_yo[b]])
                    P.op("pool", lambda e, b=b: e.tensor_tensor(
                        out=xo[b][:], in0=yo[b][:], in1=xo[b][:], op=ALU.add),
                        reads=[r_yo[b], r_xo[b]], writes=[r_xo[b]])
                    P.dma("sp", [(X[c * 128:(c + 1) * 128, n * 512:(n + 1) * 512], xo[b][:])],
                          reads=[r_xo[b]], writes=[rX[c]])
            P.barrier()

    def phase_conv(fi, chunks):
        w_in = cv_w_in[fi]
        seqs = []
        if 0 in chunks:
            seqs.append((0, NCC * 128))
        seqs.append((NCC * 128, NLAT))
        with ExitStack() as st:
            hT = sb(st, "hT", [128, KT, NT], BF16)
            r_hT = P.res("hT")
            P.dma("sp", [(hT[:, :, c * 128:(c + 1) * 128], HT[c]) for c in chunks],
                  reads=[rHT[c] for c in chunks], writes=[r_hT])
            cw = sb(st, "cw", [128, 32, 4], F32)
            r_cw = P.res("cw")
            P.dma("sp", [(cw[:], cv_cw[fi])], writes=[r_cw])
            NP = NT + 4
            Pb = [sb(st, "Pb%d" % i, [128, NP], F32) for i in range(2)]
            BZ = [sb(st, "BZ%d" % i, [128, NT], F32) for i in range(2)]
            r_Pb = [P.res("Pb%d" % i) for i in range(2)]
            r_BZ = [P.res("BZ%d" % i) for i in range(2)]
            for i in range(2):
                P.op("pool", lambda e, i=i: e.memset(Pb[i][:], 0.0), writes=[r_Pb[i]])
            tA = [sb(st, "tA%d" % i, [128, 512], F32) for i in range(2)]
            tB = [sb(st, "tB%d" % i, [128, 512], F32) for i in range(2)]
            tC = [sb(st, "tC%d" % i, [128, 512], F32) for i in range(2)]
            gto = [sb(st, "gto%d" % i, [128, NT], BF16) for i in range(2)]
            r_tA = [P.res("tA%d" % i) for i in range(2)]
            r_tB = [P.res("tB%d" % i) for i in range(2)]
            r_tC = [P.res("tC%d" % i) for i in range(2)]
            r_gto = [P.res("gto%d" % i) for i in range(2)]
            blocks = []
            for si, (t0, n) in enumerate(seqs):
                poff = 1 + 2 * si
                for b0 in range(0, n, 512):
                    blocks.append((t0 + b0, min(512, n - b0), poff))
            k_ = 0
            for f in range(32):
                fb = f % 2
                bi = next_wb()
                load_wblock(bi, [(s_ * 128, 128, w_in[:, s_ * 4096 + f * 128: s_ * 4096 + (f + 1) * 128])
                                 for s_ in range(4)])
                for (t0, n, poff) in blocks:
                    k_ += 1
                    kb = k_ % 2
                    pis = [next_ps() for _ in range(4)]
                    for s_ in range(4):
                        def mm(e, s_=s_, pi=pis[s_], bi=bi, t0=t0, n=n):
                            r = None
                            for kt in range(KT):
                                r = e.matmul(PS[pi][:, 0:n], lhsT=WB[bi][:, kt, s_ * 128:(s_ + 1) * 128],
                                             rhs=hT[:, kt, t0:t0 + n], start=(kt == 0), stop=(kt == KT - 1))
                            return r
                        P.op("pe", mm, reads=[rWB[bi], r_hT], writes=[rPS[pis[s_]]])
                    P.op("act", lambda e, kb=kb, pi=pis[2], n=n: e.copy(out=tA[kb][:, 0:n], in_=PS[pi][:, 0:n]),
                         reads=[rPS[pis[2]]], writes=[r_tA[kb]])
                    P.op("dve", lambda e, kb=kb, pi=pis[1], n=n, t0=t0, poff=poff, fb=fb: e.tensor_tensor(
                        out=Pb[fb][:, t0 + poff:t0 + poff + n], in0=PS[pi][:, 0:n], in1=tA[kb][:, 0:n], op=ALU.mult),
                        reads=[rPS[pis[1]], r_tA[kb]], writes=[r_Pb[fb]])
                    P.op("act", lambda e, kb=kb, pi=pis[3], n=n: e.activation(
                        out=tB[kb][:, 0:n], in_=PS[pi][:, 0:n], func=AF.Silu),
                        reads=[rPS[pis[3]]], writes=[r_tB[kb]])
                    P.op("dve", lambda e, kb=kb, pi=pis[0], n=n, t0=t0, fb=fb: e.tensor_tensor(
                        out=BZ[fb][:, t0:t0 + n], in0=PS[pi][:, 0:n], in1=tB[kb][:, 0:n], op=ALU.mult),
                        reads=[rPS[pis[0]], r_tB[kb]], writes=[r_BZ[fb]])
                for (t0, n, poff) in blocks:
                    k_ += 1
                    kb = k_ % 2
                    p0 = t0 + poff
                    P.op("act", lambda e, kb=kb, n=n, p0=p0, fb=fb, f=f: e.activation(
                        out=tC[kb][:, 0:n], in_=Pb[fb][:, p0:p0 + n], func=AF.Identity,
                        bias=cw[:, f, 3:4], scale=cw[:, f, 1:2]),
                        reads=[r_Pb[fb], r_cw], writes=[r_tC[kb]])
                    P.op("dve", lambda e, kb=kb, n=n, p0=p0, fb=fb, f=f: e.scalar_tensor_tensor(
                        out=tC[kb][:, 0:n], in0=Pb[fb][:, p0 - 1:p0 - 1 + n], scalar=cw[:, f, 0:1],
                        in1=tC[kb][:, 0:n], op0=ALU.mult, op1=ALU.add),
                        reads=[r_Pb[fb], r_cw, r_tC[kb]], writes=[r_tC[kb]])
                    P.op("dve", lambda e, kb=kb, n=n, p0=p0, fb=fb, f=f: e.scalar_tensor_tensor(
                        out=tC[kb][:, 0:n], in0=Pb[fb][:, p0 + 1:p0 + 1 + n], scalar=cw[:, f, 2:3],
                        in1=tC[kb][:, 0:n], op0=ALU.mult, op1=ALU.add),
                        reads=[r_Pb[fb], r_cw, r_tC[kb]], writes=[r_tC[kb]])
                    P.op("pool", lambda e, kb=kb, n=n, t0=t0, fb=fb: e.tensor_tensor(
                        out=gto[fb][:, t0:t0 + n], in0=tC[kb][:, 0:n], in1=BZ[fb][:, t0:t0 + n], op=ALU.mult),
                        reads=[r_tC[kb], r_BZ[fb]], writes=[r_gto[fb]])
                P.dma("sp", [(GT[c, :, f, :], gto[fb][:, c * 128:(c + 1) * 128]) for c in chunks],
                      reads=[r_gto[fb]], writes=[rGT[c] for c in chunks])
            P.barrier()

    def gelu_ops(src_ps, r_src, n, dst, r_dst, tmps, r_tmps, post=None):
        a, b_ = tmps
        ra, rb = r_tmps
        P.op("act", lambda e: e.activation(out=a[:, 0:n], in_=src_ps[:, 0:n], func=AF.Square),
             reads=[r_src], writes=[ra])
        P.op("dve", lambda e: e.tensor_scalar(out=a[:, 0:n], in0=a[:, 0:n], scalar1=0.044715, scalar2=1.0,
                                              op0=ALU.mult, op1=ALU.add), reads=[ra], writes=[ra])
        P.op("dve", lambda e: e.tensor_tensor(out=a[:, 0:n], in0=src_ps[:, 0:n], in1=a[:, 0:n], op=ALU.mult),
             reads=[ra, r_src], writes=[ra])
        P.op("act", lambda e: e.activation(out=b_[:, 0:n], in_=a[:, 0:n], func=AF.Sigmoid, scale=1.5957691216057308),
             reads=[ra], writes=[rb])
        P.op("dve", lambda e: e.tensor_tensor(out=dst, in0=src_ps[:, 0:n], in1=b_[:, 0:n], op=ALU.mult),
             reads=[rb, r_src], writes=[r_dst])

    def phase_gmlp(fi, chunks):
        w_in = gm_w_in[fi]
        k_ = [0]

        def common(st):
            hT = sb(st, "hT", [128, KT, NT], BF16)
            r_hT = P.res("hT")
            P.dma("sp", [(hT[:, :, c * 128:(c + 1) * 128], HT[c]) for c in chunks],
                  reads=[rHT[c] for c in chunks], writes=[r_hT])
            tmpa = [sb(st, "ga%d" % i, [128, 512], F32) for i in range(2)]
            tmpb = [sb(st, "gb_%d" % i, [128, 512], F32) for i in range(2)]
            r_ta = [P.res("ga%d" % i) for i in range(2)]
            r_tb = [P.res("gb_%d" % i) for i in range(2)]
            return hT, r_hT, tmpa, tmpb, r_ta, r_tb

        with ExitStack() as st:
            hT, r_hT, tmpa, tmpb, r_ta, r_tb = common(st)
            vo = [sb(st, "vo%d" % i, [128, 512], BF16) for i in range(2)]
            r_vo = [P.res("vo%d" % i) for i in range(2)]
            for n in range(8):
                bi = next_wb()
                load_wblock(bi, [(0, 512, w_in[:, 4096 + n * 512:4096 + (n + 1) * 512])])
                for c in chunks:
                    k_[0] += 1
                    kb = k_[0] % 2
                    pi = next_ps()

                    def mm(e, pi=pi, bi=bi, c=c, hT=hT):
                        r = None
                        for kt in range(KT):
                            r = e.matmul(PS[pi][:], lhsT=hT[:, kt, c * 128:(c + 1) * 128], rhs=WB[bi][:, kt, :],
                                         start=(kt == 0), stop=(kt == KT - 1))
                        return r
                    P.op("pe", mm, reads=[rWB[bi], r_hT], writes=[rPS[pi]])
                    gelu_ops(PS[pi], rPS[pi], 512, vo[kb][:], r_vo[kb], (tmpa[kb], tmpb[kb]), (r_ta[kb], r_tb[kb]))
                    P.dma("sp", [(VG[c, :, n * 512:(n + 1) * 512], vo[kb][:])], reads=[r_vo[kb]], writes=[rVG[c]])
            P.barrier()
        with ExitStack() as st:
            vgb = sb(st, "vgb", [128, 4096], F32)
            wsT = sb(st, "wsT", [128, 8, 128], BF16)
            bsb = sb(st, "bsb", [128, 8, 128], F32)
            r_cst = P.res("gmc")
            P.dma("sp", [(vgb[:], gm_v_g[fi:fi + 1, :].partition_broadcast(128)),
                         (bsb[:].rearrange("p g i -> p (g i)"), gm_b_s[fi:fi + 1, :].partition_broadcast(128))],
                  writes=[r_cst])
            P.dma("pool", [(wsT[:], gm_wsT[fi])], writes=[r_cst])
            vin = [sb(st, "vin%d" % i, [128, 4096], BF16) for i in range(2)]
            vnf = [sb(st, "vnf%d" % i, [128, 4096], F32) for i in range(2)]
            vnb = [sb(st, "vnb%d" % i, [128, 4096], BF16) for i in range(2)]
            so = [sb(st, "so%d" % i, [128, 4, 128], BF16) for i in range(2)]
            sst = [sb(st, "sst%d" % i, [128, 8, 6], F32) for i in range(2)]
            smv = [sb(st, "smv%d" % i, [128, 4], F32) for i in range(2)]
            r_vin = [P.res("vin%d" % i) for i in range(2)]
            r_vnf = [P.res("vnf%d" % i) for i in range(2)]
            r_vnb = [P.res("vnb%d" % i) for i in range(2)]
            r_so = [P.res("so%d" % i) for i in range(2)]
            r_sst = [P.res("sst%d" % i) for i in range(2)]
            for n_, c in enumerate(chunks):
                b = n_ % 2
                P.dma("sp", [(vin[b][:], VG[c])], reads=[rVG[c]], writes=[r_vin[b]])
                ln_stats(vin[b], 8, sst[b], smv[b], r_vin[b], r_sst[b])
                P.op("dve", lambda e, b=b: e.tensor_scalar(
                    out=vnf[b][:], in0=vin[b][:], scalar1=smv[b][:, 0:1], scalar2=smv[b][:, 2:3],
                    op0=ALU.subtract, op1=ALU.mult), reads=[r_vin[b], r_sst[b]], writes=[r_vnf[b]])
                P.op("pool", lambda e, b=b: e.tensor_tensor(out=vnb[b][:], in0=vnf[b][:], in1=vgb[:], op=ALU.mult),
                     reads=[r_vnf[b], r_cst], writes=[r_vnb[b]])
                for g in range(8):
                    k_[0] += 1
                    kb = k_[0] % 2
                    pi = next_ps()

                    def mm(e, b=b, g=g, pi=pi):
                        r = None
                        for q in range(4):
                            ft = g * 4 + q
                            r = e.matmul(PS[pi][:, q * 128:(q + 1) * 128], lhsT=vnb[b][:, ft * 128:(ft + 1) * 128],
                                         rhs=wsT[:, g, :], start=True, stop=True)
                        return r
                    P.op("pe", mm, reads=[r_vnb[b], r_cst], writes=[rPS[pi]])
                    P.op("dve", lambda e, g=g, pi=pi, kb=kb: e.tensor_tensor(
                        out=so[kb][:], in0=PS[pi][:].rearrange("p (q i) -> p q i", q=4),
                        in1=bsb[:, g:g + 1, :].to_broadcast([128, 4, 128]), op=ALU.add),
                        reads=[rPS[pi], r_cst], writes=[r_so[kb]])
                    P.dma("sp", [(ST[c, :, g * 4:(g + 1) * 4, :], so[kb][:])], reads=[r_so[kb]], writes=[rST[c]])
            P.barrier()
        with ExitStack() as st:
            hT, r_hT, tmpa, tmpb, r_ta, r_tb = common(st)
            blocks = [(b0, min(512, NT - b0)) for b0 in range(chunks[0] * 128, NT, 512)]
            sti = [sb(st, "sti%d" % i, [128, 4, 128], BF16) for i in range(2)]
            szt = [sb(st, "szt%d" % i, [128, 512], F32) for i in range(2)]
            gut = [sb(st, "gut%d" % i, [128, 512], F32) for i in range(2)]
            gto = [sb(st, "gto%d" % i, [128, 4, 128], BF16) for i in range(2)]
            r_sti = [P.res("sti%d" % i) for i in range(2)]
            r_szt = [P.res("szt%d" % i) for i in range(2)]
            r_gut = [P.res("gut%d" % i) for i in range(2)]
            r_gto = [P.res("gto%d" % i) for i in range(2)]
            for f in range(32):
                bi = next_wb()
                load_wblock(bi, [(0, 128, w_in[:, f * 128:(f + 1) * 128]),
                                 (128, 128, w_in[:, 8192 + f * 128:8192 + (f + 1) * 128])])
                for (t0, n) in blocks:
                    k_[0] += 1
                    kb = k_[0] % 2
                    c0 = t0 // 128
                    ncb = n // 128
                    P.dma("sp", [(sti[kb][:, j, :], ST[c0 + j, :, f, :]) for j in range(ncb)],
                          reads=[rST[c0 + j] for j in range(ncb)], writes=[r_sti[kb]])
                    pu, pz = next_ps(), next_ps()
                    for (pi, s_) in ((pu, 0), (pz, 1)):
                        def mm(e, pi=pi, s_=s_, bi=bi, t0=t0, n=n, hT=hT):
                            r = None
                            for kt in range(KT):
                                r = e.matmul(PS[pi][:, 0:n], lhsT=WB[bi][:, kt, s_ * 128:(s_ + 1) * 128],
                                             rhs=hT[:, kt, t0:t0 + n], start=(kt == 0), stop=(kt == KT - 1))
                            return r
                        P.op("pe", mm, reads=[rWB[bi], r_hT], writes=[rPS[pi]])
                    gelu_ops(PS[pu], rPS[pu], n, gut[kb][:, 0:n], r_gut[kb], (tmpa[kb], tmpb[kb]), (r_ta[kb], r_tb[kb]))
                    P.op("act", lambda e, kb=kb, pz=pz, n=n: e.activation(
                        out=szt[kb][:, 0:n], in_=PS[pz][:, 0:n], func=AF.Silu), reads=[rPS[pz]], writes=[r_szt[kb]])
                    P.op("pool", lambda e, kb=kb, n=n: e.tensor_tensor(
                        out=szt[kb][:, 0:n], in0=szt[kb][:, 0:n],
                        in1=sti[kb][:].rearrange("p q i -> p (q i)")[:, 0:n], op=ALU.mult),
                        reads=[r_szt[kb], r_sti[kb]], writes=[r_szt[kb]])
                    P.op("dve", lambda e, kb=kb, n=n: e.tensor_tensor(
                        out=gto[kb][:].rearrange("p q i -> p (q i)")[:, 0:n], in0=gut[kb][:, 0:n],
                        in1=szt[kb][:, 0:n], op=ALU.mult), reads=[r_gut[kb], r_szt[kb]], writes=[r_gto[kb]])
                    P.dma("sp", [(GT[c0 + j, :, f, :], gto[kb][:, j, :]) for j in range(ncb)],
                          reads=[r_gto[kb]], writes=[rGT[c0 + j] for j in range(ncb)])
            P.barrier()

    def phase_ret(fi, need_ctx):
        w_in = ret_w_in[fi]
        with ExitStack() as st:
            dec = sb(st, "dec", [128, 16], F32)
            lg = sb(st, "lg", [128, 16], F32)
            pidx = sb(st, "pidx", [128, 8], F32)
            pit = sb(st, "pit", [128, 1], I32)
            TAB = sb(st, "TAB", [128, 7, 16], F32)
            mk = sb(st, "mk", [128, 2, 128], F32)
            dif = sb(st, "dif", [128, 128], F32)
            dii = sb(st, "dii", [128, 128], I32)
            r_tab = P.res("tab")
            P.dma("sp", [(dec[:], ret_decay[fi:fi + 1, :].partition_broadcast(128))], writes=[r_tab])
            P.op("pool", lambda e: e.iota(pit[:], pattern=[[0, 1]], base=0, channel_multiplier=1), writes=[r_tab])
            P.op("pool", lambda e: e.iota(dii[:], pattern=[[1, 128]], base=0, channel_multiplier=-1), reads=[r_tab], writes=[r_tab])

            def tabs1(e):
                e.tensor_copy(out=pidx[:, 5:6], in_=pit[:])
                return e.tensor_copy(out=dif[:], in_=dii[:])

            def tabs2(e):
                e.tensor_single_scalar(out=mk[:, 0, :], in_=dif[:], scalar=0.0, op=ALU.is_ge)
                e.tensor_single_scalar(out=mk[:, 1, :], in_=dif[:], scalar=0.0, op=ALU.is_le)
                e.tensor_scalar(out=pidx[:, 0:1], in0=pidx[:, 5:6], scalar1=-1.0, scalar2=-1.0, op0=ALU.mult, op1=ALU.add)
                e.tensor_scalar(out=pidx[:, 1:2], in0=pidx[:, 5:6], scalar1=1.0, scalar2=None, op0=ALU.add)
                e.tensor_scalar(out=pidx[:, 2:3], in0=pidx[:, 5:6], scalar1=-1.0, scalar2=127.0, op0=ALU.mult, op1=ALU.add)
                e.tensor_scalar(out=pidx[:, 3:4], in0=pidx[:, 5:6], scalar1=-128.0, scalar2=None, op0=ALU.add)
                return e.tensor_scalar(out=pidx[:, 4:5], in0=pidx[:, 5:6], scalar1=-1.0, scalar2=128.0, op0=ALU.mult, op1=ALU.add)
            P.seq("dve", [tabs1, tabs2], reads=[r_tab], writes=[r_tab])
            P.op("act", lambda e: e.activation(out=lg[:], in_=dec[:], func=AF.Exp, scale=-1.0), reads=[r_tab], writes=[r_tab])
            P.op("dve", lambda e: e.tensor_scalar(out=lg[:], in0=lg[:], scalar1=1.0, scalar2=None, op0=ALU.add),
                 reads=[r_tab], writes=[r_tab])
            P.op("act", lambda e: e.activation(out=lg[:], in_=lg[:], func=AF.Ln), reads=[r_tab], writes=[r_tab])
            P.op("act", lambda e: e.mul(out=lg[:], in_=lg[:], mul=-1.0), reads=[r_tab], writes=[r_tab])

            def lgs(e):
                cols = [0, 3, 1, 4, 2, 5]
                for t_ in range(6):
                    e.activation(out=TAB[:, t_, :], in_=lg[:], func=AF.Exp, scale=pidx[:, cols[t_]:cols[t_] + 1])
                return e.activation(out=TAB[:, 6, :], in_=lg[:], func=AF.Exp, scale=128.0)
            P.op("act", lgs, reads=[r_tab], writes=[r_tab])
            rope = sb(st, "rope", [128, max(NCL, 1), 2, 2, 64], F32)
            P.dma("sp", [(rope[:], ropeT)], writes=[r_tab])

            qT = sb(st, "qT", [128, 2, NT], BF16)
            kT = sb(st, "kT", [128, 2, NT], BF16)
            ktm = sb(st, "ktm", [128, NCH, 256], BF16)
            vv = sb(st, "vv", [128, NCH, 512], BF16)
            szz = sb(st, "szz", [128, NCH, 512], BF16)
            r_q = [P.res("q%d" % c) for c in range(NCH)]
            r_v = [P.res("v%d" % c) for c in range(NCH)]
            r_z = [P.res("z%d" % c) for c in range(NCH)]
            htc = [sb(st, "htc%d" % i, [128, KT, 128], BF16) for i in range(3)]
            r_htc = [P.res("htc%d" % i) for i in range(3)]
            ra = [sb(st, "ra%d" % i, [128, 512], F32) for i in range(2)]
            rb_ = [sb(st, "rb%d" % i, [128, 512], F32) for i in range(2)]
            qkr = [sb(st, "qkr%d" % i, [128, 512], BF16) for i in range(2)]
            r_ra = [P.res("ra%d" % i) for i in range(2)]
            r_qkr = [P.res("qkr%d" % i) for i in range(2)]
            S32 = sb(st, "S32", [128, 2, 512], F32)
            Sbf = [sb(st, "Sbf%d" % i, [128, 2, 512], BF16) for i in range(2)]
            r_S32 = P.res("S32")
            r_Sbf = [P.res("Sbf%d" % i) for i in range(2)]
            sTt = [sb(st, "sTt%d" % i, [128, 128], BF16) for i in range(2)]
            kst = [sb(st, "kst%d" % i, [128, 256], BF16) for i in range(2)]
            of_ = [sb(st, "of%d" % i, [128, 512], F32) for i in range(2)]
            ol = [sb(st, "ol%d" % i, [128, 512], F32) for i in range(2)]
            on = [sb(st, "on%d" % i, [128, 512], F32) for i in range(2)]
            gg = [sb(st, "gg%d" % i, [128, 512], BF16) for i in range(2)]
            gtt = [sb(st, "gtt%d" % i, [128, 4, 128], BF16) for i in range(2)]
            lst = [sb(st, "lst%d" % i, [128, 6], F32) for i in range(2)]
            lmv = [sb(st, "lmv%d" % i, [128, 4], F32) for i in range(2)]
            r_sTt = [P.res("sTt%d" % i) for i in range(2)]
            r_kst = [P.res("kst%d" % i) for i in range(2)]
            r_of = [P.res("of%d" % i) for i in range(2)]
            r_ol = [P.res("ol%d" % i) for i in range(2)]
            r_on = [P.res("on%d" % i) for i in range(2)]
            r_gg = [P.res("gg%d" % i) for i in range(2)]
            r_gtt = [P.res("gtt%d" % i) for i in range(2)]
            r_lst = [P.res("lst%d" % i) for i in range(2)]
            kk = [0]
            allc = list(range(NCH))
            for h in range(8):
                bq, bv, bz = next_wb(), next_wb(), next_wb()
                load_wblock(bq, [(0, 256, w_in[:, h * 256:(h + 1) * 256]),
                                 (256, 256, w_in[:, 2048 + h * 256:2048 + (h + 1) * 256])])
                load_wblock(bv, [(0, 512, w_in[:, 4096 + h * 512:4096 + (h + 1) * 512])])
                load_wblock(bz, [(0, 512, w_in[:, 8192 + h * 512:8192 + (h + 1) * 512])])
                for c in allc:
                    kk[0] += 1
                    kb = kk[0] % 2
                    hb3 = kk[0] % 3
                    P.dma("sp", [(htc[hb3][:], HT[c])], reads=[rHT[c]], writes=[r_htc[hb3]])
                    pq, pv, pz = next_ps(), next_ps(), next_ps()
                    for (pi, bi) in ((pq, bq), (pv, bv), (pz, bz)):
                        if pi == pz and (c < NCC and not need_ctx):
                            continue

                        def mm(e, pi=pi, bi=bi, hb3=hb3):
                            r = None
                            for kt in range(KT):
                                r = e.matmul(PS[pi][:], lhsT=htc[hb3][:, kt, :], rhs=WB[bi][:, kt, :],
                                             start=(kt == 0), stop=(kt == KT - 1))
                            return r
                        P.op("pe", mm, reads=[r_htc[hb3], rWB[bi]], writes=[rPS[pi]])
                    P.op("act", lambda e, c=c, pv=pv: e.copy(out=vv[:, c, :], in_=PS[pv][:]),
                         reads=[rPS[pv]], writes=[r_v[c]])
                    if not (c < NCC and not need_ctx):
                        P.op("act", lambda e, c=c, pz=pz: e.activation(out=szz[:, c, :], in_=PS[pz][:], func=AF.Silu),
                             reads=[rPS[pz]], writes=[r_z[c]])
                    if c >= NCC:
                        lc = c - NCC

                        def v4(t_, w_):
                            return t_[:, w_ * 256:(w_ + 1) * 256].rearrange("p (a b f) -> p a b f", a=2, b=2)

                        def rp1(e, kb=kb, pq=pq, lc=lc):
                            r = None
                            for w_ in range(2):
                                cs_ = rope[:, lc, 0, :, :].unsqueeze(2).to_broadcast([128, 2, 2, 64])
                                sn_ = rope[:, lc, 1, :, :].unsqueeze(2).to_broadcast([128, 2, 2, 64])
                                e.tensor_tensor(out=v4(ra[kb], w_), in0=v4(PS[pq], w_), in1=cs_, op=ALU.mult)
                                r = e.tensor_tensor(out=v4(rb_[kb], w_), in0=v4(PS[pq], w_), in1=sn_, op=ALU.mult)
                            return r

                        def rp2(e, kb=kb):
                            r = None
                            for w_ in range(2):
                                A_, B_, O_ = v4(ra[kb], w_), v4(rb_[kb], w_), v4(qkr[kb], w_)
                                e.tensor_tensor(out=O_[:, :, 0, :], in0=A_[:, :, 0, :], in1=B_[:, :, 1, :], op=ALU.subtract)
                                r = e.tensor_tensor(out=O_[:, :, 1, :], in0=B_[:, :, 0, :], in1=A_[:, :, 1, :], op=ALU.add)
                            return r
                        P.seq("dve", [rp1, rp2], reads=[rPS[pq], r_tab, r_ra[kb]], writes=[r_qkr[kb], r_ra[kb]])
                    else:
                        P.op("dve", lambda e, kb=kb, pq=pq: e.tensor_copy(out=qkr[kb][:], in_=PS[pq][:]),
                             reads=[rPS[pq]], writes=[r_qkr[kb]])
                    P.op("act", lambda e, kb=kb, c=c: e.mul(out=ktm[:, c, :], in_=qkr[kb][:, 256:512], mul=0.0625),
                         reads=[r_qkr[kb]], writes=[r_q[c]])

                    def tr(e, kb=kb):
                        r = None
                        for q in range(4):
                            r = e.transpose(PTb[:, q * 128:(q + 1) * 128], qkr[kb][:, q * 128:(q + 1) * 128], ident[:])
                        return r
                    P.op("pe", tr, reads=[r_qkr[kb], r_ident], writes=[rPT[0]])
                    P.op("act", lambda e, c=c: e.copy(
                        out=qT[:, :, c * 128:(c + 1) * 128], in_=PTb[:, 0:256].rearrange("p (a t) -> p a t", a=2)),
                        reads=[rPT[0]], writes=[r_q[c]])
                    P.op("act", lambda e, c=c: e.mul(
                        out=kT[:, :, c * 128:(c + 1) * 128], in_=PTb[:, 256:512].rearrange("p (a t) -> p a t", a=2),
                        mul=0.0625), reads=[rPT[0]], writes=[r_q[c]])
                for d in range(2):
                    if d == 0:
                        order = allc
                    else:
                        order = list(range(NCC - 1, -1, -1)) + list(range(NCH - 1, NCC - 1, -1))
                    col = d * 8 + h
                    have_state = False
                    sb_i = 0
                    for c in order:
                        kk[0] += 1
                        kb = kk[0] % 2
                        cs = slice(c * 128, (c + 1) * 128)
                        last = (c == order[-1])
                        want_out = need_ctx or c >= NCC
                        if want_out:
                            psc = next_ps()

                            def mm1(e, psc=psc, cs=cs):
                                e.matmul(PS[psc][:, 0:128], lhsT=kT[:, 0, cs], rhs=qT[:, 0, cs], start=True, stop=False)
                                return e.matmul(PS[psc][:, 0:128], lhsT=kT[:, 1, cs], rhs=qT[:, 1, cs], start=False, stop=True)
                            P.op("pe", mm1, reads=[r_q[c]], writes=[rPS[psc]])
                            P.op("dve", lambda e, kb=kb, psc=psc, d=d, col=col: e.scalar_tensor_tensor(
                                out=sTt[kb][:], in0=PS[psc][:, 0:128], scalar=TAB[:, d, col:col + 1],
                                in1=mk[:, d, :], op0=ALU.mult, op1=ALU.mult),
                                reads=[rPS[psc], r_tab], writes=[r_sTt[kb]])
                            po = next_ps()

                            def mm2(e, kb=kb, po=po, c=c, cs=cs, hs=have_state, sb_i=sb_i):
                                r = e.matmul(PS[po][:], lhsT=sTt[kb][:], rhs=vv[:, c, :], start=True, stop=not hs)
                                if hs:
                                    e.matmul(PS[po][:], lhsT=qT[:, 0, cs], rhs=Sbf[sb_i][:, 0, :], start=False, stop=False)
                                    r = e.matmul(PS[po][:], lhsT=qT[:, 1, cs], rhs=Sbf[sb_i][:, 1, :], start=False, stop=True)
                                return r
                            P.op("pe", mm2, reads=[r_sTt[kb], r_v[c], r_q[c]] + ([r_Sbf[sb_i]] if have_state else []),
                                 writes=[rPS[po]])
                            if d == 0:
                                P.op("act", lambda e, kb=kb, po=po, col=col: e.activation(
                                    out=of_[kb][:], in_=PS[po][:], func=AF.Identity, scale=TAB[:, 2, col:col + 1]),
                                    reads=[rPS[po], r_tab], writes=[r_of[kb]])
                                P.dma("sp", [(OFS[c], of_[kb][:])], reads=[r_of[kb]], writes=[rOFS[c]])
                            else:
                                P.dma("sp", [(of_[kb][:], OFS[c])], reads=[rOFS[c]], writes=[r_of[kb]])
                                P.op("dve", lambda e, kb=kb, po=po, col=col: e.scalar_tensor_tensor(
                                    out=ol[kb][:], in0=PS[po][:], scalar=TAB[:, 3, col:col + 1], in1=of_[kb][:],
                                    op0=ALU.mult, op1=ALU.add), reads=[rPS[po], r_tab, r_of[kb]], writes=[r_ol[kb]])

                                ln_stats(ol[kb], 1, lst[kb], lmv[kb], r_ol[kb], r_lst[kb])
                                P.op("dve", lambda e, kb=kb: e.tensor_scalar(
                                    out=on[kb][:], in0=ol[kb][:], scalar1=lmv[kb][:, 0:1], scalar2=lmv[kb][:, 2:3],
                                    op0=ALU.subtract, op1=ALU.mult), reads=[r_ol[kb], r_lst[kb]], writes=[r_on[kb]])
                                P.op("pool", lambda e, kb=kb, c=c: e.tensor_tensor(
                                    out=gg[kb][:], in0=on[kb][:], in1=szz[:, c, :], op=ALU.mult),
                                    reads=[r_on[kb], r_z[c]], writes=[r_gg[kb]])

                                def tr2(e, kb=kb):
                                    r = None
                                    for q in range(4):
                                        r = e.transpose(PTb[:, 512 + q * 128:512 + (q + 1) * 128],
                                                        gg[kb][:, q * 128:(q + 1) * 128], ident[:])
                                    return r
                                P.op("pe", tr2, reads=[r_gg[kb], r_ident], writes=[rPT[1]])
                                P.op("act", lambda e, kb=kb: e.copy(
                                    out=gtt[kb][:], in_=PTb[:, 512:1024].rearrange("p (q t) -> p q t", q=4)),
                                    reads=[rPT[1]], writes=[r_gtt[kb]])
                                P.dma("sp", [(GT[c, :, h * 4:(h + 1) * 4, :], gtt[kb][:])],
                                      reads=[r_gtt[kb]], writes=[rGT[c]])
                        if not last:
                            P.op("act", lambda e, kb=kb, c=c, d=d, col=col: e.activation(
                                out=kst[kb][:], in_=ktm[:, c, :], func=AF.Identity, scale=TAB[:, 4 + d, col:col + 1]),
                                reads=[r_q[c], r_tab], writes=[r_kst[kb]])
                            pus = [next_ps(), next_ps()]
                            for dt_ in range(2):
                                P.op("pe", lambda e, kb=kb, c=c, dt_=dt_, pu=pus[dt_]: e.matmul(
                                    PS[pu][:], lhsT=kst[kb][:, dt_ * 128:(dt_ + 1) * 128], rhs=vv[:, c, :],
                                    start=True, stop=True), reads=[r_kst[kb], r_v[c]], writes=[rPS[pus[dt_]]])
                            nsb = 1 - sb_i
                            for dt_ in range(2):
                                if have_state:
                                    P.op("dve", lambda e, dt_=dt_, pu=pus[dt_], col=col: e.scalar_tensor_tensor(
                                        out=S32[:, dt_, :], in0=S32[:, dt_, :], scalar=TAB[:, 6, col:col + 1],
                                        in1=PS[pu][:], op0=ALU.mult, op1=ALU.add),
                                        reads=[rPS[pus[dt_]], r_S32, r_tab], writes=[r_S32])
                                else:
                                    P.op("dve", lambda e, dt_=dt_, pu=pus[dt_]: e.tensor_copy(
                                        out=S32[:, dt_, :], in_=PS[pu][:]), reads=[rPS[pus[dt_]]], writes=[r_S32])
                            P.op("act", lambda e, nsb=nsb: e.copy(out=Sbf[nsb][:], in_=S32[:]),
                                 reads=[r_S32], writes=[r_Sbf[nsb]])
                            sb_i = nsb
                            have_state = True
            P.barrier()

    def phase_final():
        with ExitStack() as st:
            fg = sb(st, "fg", [128, D], F32)
            r_fg = P.res("fg")
            P.dma("sp", [(fg[:], final_g.partition_broadcast(128))], writes=[r_fg])
            xt = [sb(st, "fxt%d" % i, [128, D], F32) for i in range(2)]
            yt = [sb(st, "fyt%d" % i, [128, D], F32) for i in range(2)]
            stt = [sb(st, "fst%d" % i, [128, 4, 6], F32) for i in range(2)]
            mv = [sb(st, "fmv%d" % i, [128, 4], F32) for i in range(2)]
            r_xt = [P.res("fxt%d" % i) for i in range(2)]
            r_yt = [P.res("fyt%d" % i) for i in range(2)]
            r_st = [P.res("fst%d" % i) for i in range(2)]
            for lc in range(NCL):
                c = NCC + lc
                b = lc % 2
                P.dma("sp", [(xt[b][:], X[c * 128:(c + 1) * 128, :])], reads=[rX[c]], writes=[r_xt[b]])

                rms_stats(xt[b], stt[b], mv[b], r_xt[b], r_st[b])
                P.op("dve", lambda e, b=b: e.scalar_tensor_tensor(
                    out=yt[b][:], in0=xt[b][:], scalar=mv[b][:, 3:4], in1=fg[:], op0=ALU.mult, op1=ALU.mult),
                    reads=[r_xt[b], r_st[b], r_fg], writes=[r_yt[b]])
                P.dma("sp", [(out[lc * 128:(lc + 1) * 128, :], yt[b][:])], reads=[r_yt[b]], writes=[])

    for li, (kind, fi, need_ctx) in enumerate(layers):
        chunks = list(range(NCH)) if need_ctx else list(range(NCC, NCH))
        phase_mod(li)
        phase_norm(li, list(range(NCH)) if (need_ctx or kind == 0) else chunks)
        if kind == 0:
            phase_ret(fi, need_ctx)
            phase_out(ret_w_out[fi], chunks)
        elif kind == 1:
            phase_gmlp(fi, chunks)
            phase_out(gm_w_out[fi], chunks)
        else:
            phase_conv(fi, chunks)
            phase_out(cv_w_out[fi], chunks)
    if debug_x:
        for c in range(NCH):
            P.dma("sp", [(xdbg[c * 128:(c + 1) * 128, :], X[c * 128:(c + 1) * 128, :])], reads=[rX[c]])
    if final_norm:
        phase_final()
    else:
        for lc in range(NCL):
            P.dma("sp", [(out[lc * 128:(lc + 1) * 128, :], X[(NCC + lc) * 128:(NCC + lc + 1) * 128, :])],
                  reads=[rX[NCC + lc]])
    P.emit(top)
    top.close()
    return nc


def rope_tables(NCL):
    half = 64
    freqs = (10000.0 ** (-np.arange(half, dtype=np.float32) / half)).astype(np.float32)
    t = np.arange(NCL * 128)
    rows = (t // 64).astype(np.float32)
    cols = (t % 64).astype(np.float32)
    ar = rows[:, None] * freqs[None, :]
    ac = cols[:, None] * freqs[None, :]
    tab = np.zeros((NCL * 128, 2, 2, 64), np.float32)
    tab[:, 0, 0] = np.cos(ar)
    tab[:, 0, 1] = np.cos(ac)
    tab[:, 1, 0] = np.sin(ar)
    tab[:, 1, 1] = np.sin(ac)
    return np.ascontiguousarray(tab.reshape(NCL, 128, 2, 2, 64).transpose(1, 0, 2, 3, 4))


def make_in_maps(inp, layers, NCL, n_cores=8):
    B = inp["x"].shape[0]
    L = len(layers)
    shared = {
        "norm_g": np.ascontiguousarray(inp["norm_g"][:L], dtype=np.float32),
        "ada_w": np.ascontiguousarray(inp["ada_w"][:L], dtype=np.float32),
        "ada_b": np.ascontiguousarray(inp["ada_b"][:L], dtype=np.float32),
        "final_g": np.ascontiguousarray(inp["final_g"].reshape(1, D), dtype=np.float32),
        "ropeT": rope_tables(NCL),
    }
    kinds = [k for k, _, _ in layers]
    if 0 in kinds:
        n = max(fi for k, fi, _ in layers if k == 0) + 1
        shared["ret_w_in"] = np.ascontiguousarray(inp["ret_w_in"][:n], dtype=np.float32)
        shared["ret_w_out"] = np.ascontiguousarray(inp["ret_w_out"][:n], dtype=np.float32)
        shared["ret_decay"] = np.ascontiguousarray(inp["ret_decay"][:n].reshape(n, 16), dtype=np.float32)
    if 1 in kinds:
        n = max(fi for k, fi, _ in layers if k == 1) + 1
        shared["gm_w_in"] = np.ascontiguousarray(inp["gm_w_in"][:n], dtype=np.float32)
        shared["gm_w_out"] = np.ascontiguousarray(inp["gm_w_out"][:n], dtype=np.float32)
        shared["gm_v_g"] = np.ascontiguousarray(inp["gm_v_g"][:n], dtype=np.float32)
        shared["gm_wsT"] = np.ascontiguousarray(np.transpose(inp["gm_w_s"][:n], (0, 3, 1, 2)), dtype=np.float32)
        shared["gm_b_s"] = np.ascontiguousarray(inp["gm_b_s"][:n].reshape(n, 1024), dtype=np.float32)
    if 2 in kinds:
        n = max(fi for k, fi, _ in layers if k == 2) + 1
        shared["cv_w_in"] = np.ascontiguousarray(inp["cv_w_in"][:n], dtype=np.float32)
        shared["cv_w_out"] = np.ascontiguousarray(inp["cv_w_out"][:n], dtype=np.float32)
        cw = np.concatenate([inp["cv_conv_w"][:n], inp["cv_conv_b"][:n][:, None, :]], axis=1)
        cw = cw.reshape(n, 4, 32, 128).transpose(0, 3, 2, 1)
        shared["cv_cw"] = np.ascontiguousarray(cw, dtype=np.float32)
    maps = []
    for core in range(n_cores):
        b = core % B
        m = dict(shared)
        m["xin"] = np.ascontiguousarray(np.concatenate([inp["ctx"][b], inp["x"][b]], axis=0), dtype=np.float32)
        cc = np.concatenate([inp["c"][b].reshape(KT, 128).T, inp["c_ctx"].reshape(KT, 128).T], axis=1)
        m["cT"] = np.ascontiguousarray(cc, dtype=np.float32)
        maps.append(m)
    return maps


N_CORES = 4
FULL_LAYERS = [(0, 0, True), (1, 0, True), (2, 0, True), (0, 1, False)]


def kernel(**inputs):
    inp = {k: np.asarray(v) for k, v in inputs.items()}
    B, T, _ = inp["x"].shape
    NCL = T // 128
    nc = build(NCL, FULL_LAYERS)
    n_cores = N_CORES
    maps = make_in_maps(inp, FULL_LAYERS, NCL, n_cores=n_cores)
    res = run_bass_kernel_spmd(nc, maps, core_ids=list(range(n_cores)))
    outs = [np.asarray(res.results[b]["out"], dtype=np.float32) for b in range(B)]
    return np.stack(outs, axis=0)
```
